# Optimizing a Trainium2 kernel written in Bass

```python
import math
import jax
import jax.numpy as jnp
from jax import lax
import numpy as np

D_MODEL = 2048
BATCH = 8
SEQ = 4096
DEPTH = 4

ATTN_HEADS = 8
QK_NOPE_DIM = 128
QK_ROPE_DIM = 64
QK_HEAD_DIM = QK_NOPE_DIM + QK_ROPE_DIM
V_HEAD_DIM = 128
Q_LORA_RANK = 512
KV_LORA_RANK = 512
ATTN_WIDTH = ATTN_HEADS * V_HEAD_DIM
ROPE_THETA = 10000.0
Q_BLOCK = 128

LRU_WIDTH = D_MODEL - ATTN_WIDTH
LRU_BLOCKS = 8
LRU_BLOCK_DIM = LRU_WIDTH // LRU_BLOCKS
LRU_C = 8.0
CONV_WIDTH = 4
CONV_LEFT = 2
N_DIRS = 2

D_FF = 4 * D_MODEL

LN_EPS = 1e-5
RMS_EPS = 1e-6
DEEPNORM_ALPHA = (2.0 * DEPTH) ** 0.25
DEEPNORM_BETA = (8.0 * DEPTH) ** -0.25

IN_WIDTH = Q_LORA_RANK + KV_LORA_RANK + QK_ROPE_DIM + 2 * LRU_WIDTH
SPLIT_POINTS = (
    Q_LORA_RANK,
    Q_LORA_RANK + KV_LORA_RANK,
    Q_LORA_RANK + KV_LORA_RANK + QK_ROPE_DIM,
    Q_LORA_RANK + KV_LORA_RANK + QK_ROPE_DIM + LRU_WIDTH,
)

kernel_name = 'bidir_hybrid_mla_rglru_deepnorm'


def layer_norm(x, g, b):
    xf = x.astype(jnp.float32)
    mu = jnp.mean(xf, axis=-1, keepdims=True)
    xc = xf - mu
    var = jnp.mean(xc * xc, axis=-1, keepdims=True)
    return (xc * lax.rsqrt(var + LN_EPS) * g.astype(jnp.float32) + b.astype(jnp.float32)).astype(x.dtype)


def rms_norm(x, g):
    xf = x.astype(jnp.float32)
    ms = jnp.mean(xf * xf, axis=-1, keepdims=True)
    return (xf * lax.rsqrt(ms + RMS_EPS) * g.astype(jnp.float32)).astype(x.dtype)


def rope_tables(positions):
    inv_freq = ROPE_THETA ** (-jnp.arange(0, QK_ROPE_DIM, 2, dtype=jnp.float32) / QK_ROPE_DIM)
    ang = positions.astype(jnp.float32)[..., None] * inv_freq
    return jnp.cos(ang), jnp.sin(ang)


def apply_rotary(x, cos, sin):
    xf = x.astype(jnp.float32)
    x1, x2 = jnp.split(xf, 2, axis=-1)
    return jnp.concatenate([x1 * cos - x2 * sin, x2 * cos + x1 * sin], axis=-1).astype(x.dtype)


def mla_heads(c_q, c_kv, k_pe, cos, sin, q_norm_g, kv_norm_g, w_uq, w_ukv):
    B, S, _ = c_q.shape
    q = (rms_norm(c_q, q_norm_g) @ w_uq).reshape(B, S, ATTN_HEADS, QK_HEAD_DIM)
    q_nope, q_pe = jnp.split(q, [QK_NOPE_DIM], axis=-1)
    q_pe = apply_rotary(q_pe, cos[:, :, None, :], sin[:, :, None, :])
    q = jnp.concatenate([q_nope, q_pe], axis=-1)
    kv = (rms_norm(c_kv, kv_norm_g) @ w_ukv).reshape(B, S, ATTN_HEADS, QK_NOPE_DIM + V_HEAD_DIM)
    k_nope, v = jnp.split(kv, [QK_NOPE_DIM], axis=-1)
    k_pe = apply_rotary(k_pe, cos, sin)
    k = jnp.concatenate(
        [k_nope, jnp.broadcast_to(k_pe[:, :, None, :], (B, S, ATTN_HEADS, QK_ROPE_DIM))], axis=-1)
    scale = QK_HEAD_DIM ** -0.5
    n_blk = S // Q_BLOCK
    q_blocks = q.reshape(B, n_blk, Q_BLOCK, ATTN_HEADS, QK_HEAD_DIM).transpose(1, 0, 2, 3, 4)

    def attend_block(qb):
        s = jnp.einsum('bqhd,bkhd->bhqk', qb, k).astype(jnp.float32) * scale
        p = jax.nn.softmax(s, axis=-1).astype(v.dtype)
        return jnp.einsum('bhqk,bkhd->bqhd', p, v)

    o = lax.map(attend_block, q_blocks)
    return o.transpose(1, 0, 2, 3, 4).reshape(B, S, ATTN_WIDTH)


def centred_depthwise_conv(x, w, b):
    S = x.shape[1]
    xp = jnp.pad(x, ((0, 0), (CONV_LEFT, CONV_WIDTH - 1 - CONV_LEFT), (0, 0)))
    y = xp[:, 0:S] * w[0]
    for tap in range(1, CONV_WIDTH):
        y = y + xp[:, tap:tap + S] * w[tap]
    return y + b


def lru_combine(left, right):
    a1, b1 = left
    a2, b2 = right
    return a1 * a2, a2 * b1 + b2


def rg_lru_bidirectional(x, w_a, b_a, w_i, b_i, lam):
    B, S, W = x.shape
    xf = x.astype(jnp.float32)
    xb = xf.reshape(B, S, LRU_BLOCKS, LRU_BLOCK_DIM)
    outs = []
    for d in range(N_DIRS):
        r = jax.nn.sigmoid(jnp.einsum('bshi,hij->bshj', xb, w_a[d].astype(jnp.float32))
                           + b_a[d].astype(jnp.float32)).reshape(B, S, W)
        i = jax.nn.sigmoid(jnp.einsum('bshi,hij->bshj', xb, w_i[d].astype(jnp.float32))
                           + b_i[d].astype(jnp.float32)).reshape(B, S, W)
        log_a = -LRU_C * r * jax.nn.softplus(-lam[d].astype(jnp.float32))
        a = jnp.exp(log_a)
        u = jnp.sqrt(-jnp.expm1(2.0 * log_a)) * (i * xf)
        _, h = lax.associative_scan(lru_combine, (a, u), axis=1, reverse=(d == 1))
        outs.append(h)
    return (outs[0] + outs[1]).astype(x.dtype)


def setup_inputs(seed: int = 0) -> dict:
    key = jax.random.key(seed)
    ks = jax.random.split(key, 28)

    def normal(k, shape, std):
        return jax.random.normal(k, shape, jnp.float32) * std

    x = normal(ks[0], (BATCH, SEQ, D_MODEL), 1.0)
    offs = jax.random.randint(ks[1], (BATCH, 1), 0, SEQ, dtype=jnp.int32)
    positions = (jnp.arange(SEQ, dtype=jnp.int32)[None, :] + offs).astype(jnp.int32)
    ln_in_g = 1.0 + normal(ks[2], (D_MODEL,), 0.01)
    ln_in_b = normal(ks[3], (D_MODEL,), 0.01)

    w_in = normal(ks[4], (DEPTH, D_MODEL, IN_WIDTH), D_MODEL ** -0.5)
    q_norm_g = 1.0 + normal(ks[5], (DEPTH, Q_LORA_RANK), 0.01)
    kv_norm_g = 1.0 + normal(ks[6], (DEPTH, KV_LORA_RANK), 0.01)
    w_uq = normal(ks[7], (DEPTH, Q_LORA_RANK, ATTN_HEADS * QK_HEAD_DIM), Q_LORA_RANK ** -0.5)
    kv_col_scale = jnp.tile(jnp.concatenate([jnp.ones((QK_NOPE_DIM,), jnp.float32),
                                             jnp.full((V_HEAD_DIM,), DEEPNORM_BETA, jnp.float32)]),
                            ATTN_HEADS)
    w_ukv = normal(ks[8], (DEPTH, KV_LORA_RANK, ATTN_HEADS * (QK_NOPE_DIM + V_HEAD_DIM)),
                   KV_LORA_RANK ** -0.5) * kv_col_scale

    conv_w = normal(ks[9], (DEPTH, CONV_WIDTH, LRU_WIDTH), CONV_WIDTH ** -0.5)
    conv_b = normal(ks[10], (DEPTH, LRU_WIDTH), 0.01)
    rg_w_a = normal(ks[11], (DEPTH, N_DIRS, LRU_BLOCKS, LRU_BLOCK_DIM, LRU_BLOCK_DIM), LRU_BLOCK_DIM ** -0.5)
    rg_b_a = normal(ks[12], (DEPTH, N_DIRS, LRU_BLOCKS, LRU_BLOCK_DIM), 0.01)
    rg_w_i = normal(ks[13], (DEPTH, N_DIRS, LRU_BLOCKS, LRU_BLOCK_DIM, LRU_BLOCK_DIM), LRU_BLOCK_DIM ** -0.5)
    rg_b_i = normal(ks[14], (DEPTH, N_DIRS, LRU_BLOCKS, LRU_BLOCK_DIM), 0.01)
    a_c = jax.random.uniform(ks[15], (DEPTH, N_DIRS, LRU_WIDTH), jnp.float32, 0.9, 0.999)
    s = a_c ** (1.0 / LRU_C)
    rg_lambda = jnp.log(s) - jnp.log1p(-s)

    w_out = normal(ks[16], (DEPTH, D_MODEL, D_MODEL), D_MODEL ** -0.5 * DEEPNORM_BETA)
    ln1_g = 1.0 + normal(ks[17], (DEPTH, D_MODEL), 0.01)
    ln1_b = normal(ks[18], (DEPTH, D_MODEL), 0.01)
    w_up = normal(ks[19], (DEPTH, D_MODEL, D_FF), D_MODEL ** -0.5)
    w_down = normal(ks[20], (DEPTH, D_FF, D_MODEL), D_FF ** -0.5 * DEEPNORM_BETA)
    ln2_g = 1.0 + normal(ks[21], (DEPTH, D_MODEL), 0.01)
    ln2_b = normal(ks[22], (DEPTH, D_MODEL), 0.01)
    return {
        'x': x, 'positions': positions, 'ln_in_g': ln_in_g, 'ln_in_b': ln_in_b,
        'w_in': w_in, 'q_norm_g': q_norm_g, 'kv_norm_g': kv_norm_g, 'w_uq': w_uq, 'w_ukv': w_ukv,
        'conv_w': conv_w, 'conv_b': conv_b, 'rg_w_a': rg_w_a, 'rg_b_a': rg_b_a,
        'rg_w_i': rg_w_i, 'rg_b_i': rg_b_i, 'rg_lambda': rg_lambda,
        'w_out': w_out, 'ln1_g': ln1_g, 'ln1_b': ln1_b,
        'w_up': w_up, 'w_down': w_down, 'ln2_g': ln2_g, 'ln2_b': ln2_b,
    }


def reference(x, positions, ln_in_g, ln_in_b, w_in, q_norm_g, kv_norm_g, w_uq, w_ukv,
              conv_w, conv_b, rg_w_a, rg_b_a, rg_w_i, rg_b_i, rg_lambda,
              w_out, ln1_g, ln1_b, w_up, w_down, ln2_g, ln2_b):
    cos, sin = rope_tables(positions)
    h = layer_norm(x, ln_in_g, ln_in_b)
    for l in range(DEPTH):
        z = h @ w_in[l]
        c_q, c_kv, k_pe, x_lru, g_lru = jnp.split(z, SPLIT_POINTS, axis=-1)
        y_attn = mla_heads(c_q, c_kv, k_pe, cos, sin, q_norm_g[l], kv_norm_g[l], w_uq[l], w_ukv[l])
        x_conv = centred_depthwise_conv(x_lru, conv_w[l], conv_b[l])
        y_lru = jax.nn.gelu(g_lru) * rg_lru_bidirectional(
            x_conv, rg_w_a[l], rg_b_a[l], rg_w_i[l], rg_b_i[l], rg_lambda[l])
        y = jnp.concatenate([y_attn, y_lru], axis=-1) @ w_out[l]
        h = layer_norm(DEEPNORM_ALPHA * h + y, ln1_g[l], ln1_b[l])
        f = jnp.square(jax.nn.relu(h @ w_up[l])) @ w_down[l]
        h = layer_norm(DEEPNORM_ALPHA * h + f, ln2_g[l], ln2_b[l])
    return h
```

```python
import numpy as np
from contextlib import ExitStack
import concourse.bass as bass
import concourse.mybir as mybir
from concourse.bass_utils import run_bass_kernel_spmd

F32 = mybir.dt.float32
BF16 = mybir.dt.bfloat16
I32 = mybir.dt.int32
AF = mybir.ActivationFunctionType
ALU = mybir.AluOpType

S = 4096
D = 2048
L = 4
TB = 512
NTB = S // TB
ALPHA = float(8.0 ** 0.25)
LN_EPS = 1e-5
RMS_EPS = 1e-6
SCALE = float(192 ** -0.5)
LRU_C = 8.0
NCOL = 96
NCORES = 8

SZ = [("w_in", 25 * 128 * 16 * 128), ("uqn", 128 * 8 * 4 * 128), ("uqr", 128 * 4 * 4 * 256),
      ("uk", 128 * 8 * 4 * 128), ("uv", 128 * 4 * 1024), ("wout", 128 * 16 * 2048),
      ("wup", 64 * 128 * 16 * 128), ("wdn", 4 * 8 * 128 * 8 * 512), ("rg", 8 * 128 * 4 * 128)]
OFF = {}
_o = 0
for _k, _n in SZ:
    OFF[_k] = _o
    _o += _n
NW = _o


class Res:
    __slots__ = ("w", "rs")

    def __init__(self):
        self.w = {}
        self.rs = {}


class Prog:
    def __init__(self, nc, es):
        self.nc = nc
        self.ce = ["pe", "act", "dve", "pool"]
        self.qs = ["pe", "act", "dve", "pool", "sp"]
        self.sem = {e: es.enter_context(nc.semaphore("s_" + e)) for e in self.ce}
        self.cnt = {e: 0 for e in self.ce}
        self.q = {e: [] for e in self.qs}
        self.waited = {e: {} for e in self.qs}
        self.dpool = {}
        for qn, n in (("sp", 24), ("pool", 20), ("act", 6)):
            self.dpool[qn] = {"sems": [es.enter_context(nc.semaphore(f"d_{qn}{i}")) for i in range(n)],
                              "cnt": [0] * n, "rr": 0}
        self.res = []

    def R(self):
        r = Res()
        self.res.append(r)
        return r

    def _deps(self, eng, reads, writes):
        deps = {}

        def add(d):
            for num, (s, v) in d.items():
                if deps.get(num, (None, 0))[1] < v:
                    deps[num] = (s, v)
        for r in reads:
            add(r.w)
        for r in writes:
            add(r.w)
            add(r.rs)
        out = []
        pe_num = self.sem["pe"].num
        for num, (s, v) in deps.items():
            if eng == "pe" and num == pe_num:
                continue
            if self.waited[eng].get(num, 0) >= v:
                continue
            self.waited[eng][num] = v
            out.append((s, v))
        return out

    def _update(self, tk, reads, writes, partial):
        s, v = tk
        for r in writes:
            if not partial:
                r.w = {}
                r.rs = {}
            r.w[s.num] = (s, v)
        for r in reads:
            r.rs[s.num] = (s, v)

    def op(self, eng, emit, reads=(), writes=(), partial=False):
        waits = self._deps(eng, reads, writes)
        self.cnt[eng] += 1
        tk = (self.sem[eng], self.cnt[eng])
        self.q[eng].append((waits, emit, tk[0], 1))
        self._update(tk, reads, writes, partial)

    def dma(self, qeng, out, in_, reads=(), writes=(), partial=False):
        dp = self.dpool[qeng]
        j = dp["rr"]
        dp["rr"] = (j + 1) % len(dp["sems"])
        s = dp["sems"][j]
        waits = self._deps(qeng, reads, writes)
        prev = 16 * dp["cnt"][j]
        if prev > 0 and self.waited[qeng].get(s.num, 0) < prev:
            self.waited[qeng][s.num] = prev
            waits.append((s, prev))
        dp["cnt"][j] += 1
        tk = (s, 16 * dp["cnt"][j])
        self.q[qeng].append((waits, (lambda e, o=out, i=in_: e.dma_start(out=o, in_=i)), s, 16))
        self._update(tk, reads, writes, partial)

    def barrier(self):
        allsems = [(self.sem[e], self.cnt[e]) for e in self.ce]
        for dp in self.dpool.values():
            allsems += [(s, 16 * c) for s, c in zip(dp["sems"], dp["cnt"])]
        for eng in self.qs:
            waits = []
            for s, v in allsems:
                if v > 0 and self.waited[eng].get(s.num, 0) < v:
                    self.waited[eng][s.num] = v
                    waits.append((s, v))
            if waits:
                self.q[eng].append((waits, None, None, 0))
        for r in self.res:
            r.w = {}
            r.rs = {}

    def replay(self, eng, e):
        for waits, emit, s, n in self.q[eng]:
            for ws, wv in waits:
                e.wait_ge(ws, wv)
            if emit is not None:
                ins = emit(e)
                ins.then_inc(s, n)


class Arena:
    def __init__(self, t, ncols):
        self.t = t
        self.n = ncols
        self.off = 0

    def reset(self):
        self.off = 0

    def alloc(self, free, dtype, parts=128):
        n = int(np.prod(free))
        words = n if dtype != BF16 else (n + 1) // 2
        words = (words + 15) // 16 * 16
        assert self.off + words <= self.n, ("arena overflow", self.off, words, self.n)
        ap = self.t[:, self.off:self.off + words]
        self.off += words
        if dtype == BF16:
            ap = ap.bitcast(BF16)[:, 0:n]
        elif dtype == I32:
            ap = ap.bitcast(I32)[:, 0:n]
        else:
            ap = ap[:, 0:n]
        if len(free) == 2:
            ap = ap.rearrange("p (a b) -> p a b", a=free[0])
        elif len(free) == 3:
            ap = ap.rearrange("p (a b c) -> p a b c", a=free[0], b=free[1])
        return ap


def build_program(n_layers=L, debug=(), stop_after=None, nseq=1):
    nc = bass.Bass("TRN2", target_bir_lowering=False)
    es = ExitStack()
    with es:
        def din(name, shape, dt):
            return nc.dram_tensor(name, shape, dt, kind="ExternalInput").ap()

        def scratch(name, shape, dt):
            if name in debug:
                return nc.dram_tensor(name, shape, dt, kind="ExternalOutput").ap()
            return nc.dram_tensor(name, shape, dt).ap()

        x_all = din("x", [nseq, S, D], F32)
        pos_all = din("pos", [nseq, 1, S], I32)
        wf_d = [din(f"wf{l}", [NW], F32) for l in range(n_layers)]
        cols_d = din("cols", [L, 128, NCOL], F32)
        lnv_d = din("lnv", [L, 4, D], F32)
        lnin_d = din("lnin", [2, D], F32)
        cst_d = din("consts", [128, 4], F32)
        ident_d = din("ident", [128, 128], F32)
        out_all = nc.dram_tensor("out", [nseq, S, D], F32, kind="ExternalOutput").ap()

        wb_d = [scratch(f"wb{l}", [NW], BF16) for l in range(L)]
        h32_d = scratch("h32", [S, D], F32)
        hT_d = scratch("hT", [NTB, 128, 16 * TB], BF16)
        yT_d = scratch("yT", [NTB, 128, 16 * TB], BF16)
        zl_d = scratch("zl", [16, 128, S], F32)
        qn_d = scratch("qn", [8, 128, S], BF16)
        qpe_d = scratch("qpe", [512, S], BF16)
        kn_d = scratch("kn", [8, 128, S], BF16)
        kpe_d = scratch("kpe", [64, S], BF16)
        v_d = scratch("v", [8, 128, 32 * 128], BF16)
        cs_d = scratch("cs", [2, 128, S], F32)

        arena_t = es.enter_context(nc.sbuf_tensor("arena", [128, 51200], F32))
        identb = es.enter_context(nc.sbuf_tensor("identb", [128, 128], BF16))
        ones1 = es.enter_context(nc.sbuf_tensor("ones1", [128, 128], BF16))
        onesr = es.enter_context(nc.sbuf_tensor("onesr", [128, 128], BF16))
        cst = es.enter_context(nc.sbuf_tensor("cst", [128, 4], F32))
        small = es.enter_context(nc.sbuf_tensor("small", [128, 256], F32))
        psum = [es.enter_context(nc.psum_tensor(f"ps{i}", [128, 512], F32)) for i in range(8)]
        P = Prog(nc, es)
        A = Arena(arena_t, 51200)
        pres = [P.R() for _ in range(8)]
        constres = P.R()
        bank_rr = [0]

        def nbank():
            b = bank_rr[0]
            bank_rr[0] = (b + 1) % 8
            return b

        sm_rr = [0]

        def smslot(n):
            i = sm_rr[0]
            sm_rr[0] = (i + 1) % 16
            return small[:, i * 16:i * 16 + n]

        smres = [P.R() for _ in range(16)]

        def smres_cur():
            return smres[(sm_rr[0] - 1) % 16]

        wres = [P.R() for _ in range(L)]
        CH = 1 << 22
        for l in range(n_layers):
            o = 0
            while o < NW:
                n = min(CH, NW - o)
                P.dma("pool", wb_d[l][o:o + n].rearrange("(p f) -> p f", p=128),
                      wf_d[l][o:o + n].rearrange("(p f) -> p f", p=128), writes=[wres[l]], partial=True)
                o += n
        P.dma("pool", identb[:], ident_d, writes=[constres], partial=True)
        P.dma("sp", cst[:], cst_d, writes=[constres], partial=True)
        P.op("dve", lambda e: e.memset(ones1[:], 1.0), writes=[constres], partial=True)
        P.op("dve", lambda e: e.memset(onesr[:], 1.0 / 512.0), writes=[constres], partial=True)

        for sq in range(nseq):
            x_d = x_all[sq]
            pos_d = pos_all[sq]
            out_d = out_all[sq]
            A.reset()
            posi = A.alloc([S], I32)
            u_t = A.alloc([S], F32)
            k_i = A.alloc([S], I32)
            k_f = A.alloc([S], F32)
            r_t = A.alloc([S], F32)
            m_t = A.alloc([S], F32)
            tab = A.alloc([S], F32)
            tab2 = A.alloc([S], F32)
            rr = P.R()
            tr1 = P.R()
            tr2 = P.R()
            csres = P.R()
            PI = float(np.pi)
            TWO_PI = float(2 * np.pi)
            C1 = 6.28125
            C2 = float(2 * np.pi - 6.28125)
            PIC = 3.1415925
            P.dma("sp", posi, pos_d.partition_broadcast(128), writes=[rr])
            P.op("dve", lambda e: e.tensor_copy(u_t, posi), reads=[rr], writes=[rr])
            P.op("dve", lambda e: e.tensor_scalar(u_t, u_t, cst[:, 0:1], None, ALU.mult), reads=[rr, constres], writes=[rr])
            P.op("dve", lambda e: e.tensor_scalar(k_i, u_t, 1.0 / TWO_PI, None, ALU.mult), reads=[rr], writes=[rr])
            P.op("dve", lambda e: e.tensor_copy(k_f, k_i), reads=[rr], writes=[rr])
            P.op("dve", lambda e: e.scalar_tensor_tensor(r_t, k_f, -C1, u_t, ALU.mult, ALU.add), reads=[rr], writes=[rr])
            P.op("dve", lambda e: e.scalar_tensor_tensor(r_t, k_f, -C2, r_t, ALU.mult, ALU.add), reads=[rr], writes=[rr])

            def wrap(t):
                P.op("dve", lambda e: e.tensor_scalar(m_t, t, PI, TWO_PI, ALU.is_gt, ALU.mult), reads=[rr], writes=[rr])
                P.op("dve", lambda e: e.tensor_tensor(t, t, m_t, ALU.subtract), reads=[rr], writes=[rr])
                P.op("dve", lambda e: e.tensor_scalar(m_t, t, -PI, TWO_PI, ALU.is_lt, ALU.mult), reads=[rr], writes=[rr])
                P.op("dve", lambda e: e.tensor_tensor(t, t, m_t, ALU.add), reads=[rr], writes=[rr])
                P.op("dve", lambda e: e.tensor_scalar(t, t, PIC, -PIC, ALU.min, ALU.max), reads=[rr], writes=[rr])
            wrap(r_t)
            P.op("act", lambda e: e.activation(tab2, r_t, AF.Sin, scale=cst[:, 1:2]), reads=[rr, constres], writes=[tr2])
            P.dma("pool", cs_d[1], tab2, reads=[tr2], writes=[csres], partial=True)
            P.op("dve", lambda e: e.tensor_scalar(u_t, r_t, PI / 2, None, ALU.add), reads=[rr], writes=[rr])
            wrap(u_t)
            P.op("act", lambda e: e.activation(tab, u_t, AF.Sin), reads=[rr], writes=[tr1])
            P.dma("pool", cs_d[0], tab, reads=[tr1], writes=[csres], partial=True)
            P.barrier()

            def ln_tile(v, vres, gB, bB, gbres, hb, hbres, eps=LN_EPS):
                st = A_stats[0]
                sti = st_rr[0]
                st_rr[0] = (sti + 1) % 4
                stt = st[:, sti, :, :]
                sres = stres[sti]
                for c in range(4):
                    P.op("dve", lambda e, c=c: e.bn_stats(stt[:, c, :], v[:, c * 512:(c + 1) * 512]),
                         reads=[vres], writes=[sres], partial=(c > 0))
                mv = stt[:, 4, 0:2]
                rs = stt[:, 4, 2:3]
                P.op("dve", lambda e: e.bn_aggr(mv, stt[:, 0:4, :].rearrange("p a b -> p (a b)")), reads=[sres], writes=[sres])
                P.op("act", lambda e: e.activation(rs, stt[:, 4, 1:2], AF.Sqrt, bias=eps, scale=1.0), reads=[sres], writes=[sres])
                P.op("dve", lambda e: e.reciprocal(rs, rs), reads=[sres], writes=[sres])
                P.op("dve", lambda e: e.tensor_scalar(v, v, stt[:, 4, 0:1], rs, ALU.subtract, ALU.mult),
                     reads=[vres, sres], writes=[vres])
                P.op("pool", lambda e: e.tensor_tensor(v, v, gB, ALU.mult), reads=[vres, gbres], writes=[vres])
                P.op("pool", lambda e: e.tensor_tensor(v, v, bB, ALU.add), reads=[vres, gbres], writes=[vres])
                if hb is not None:
                    P.op("act", lambda e: e.activation(hb, v, AF.Copy), reads=[vres], writes=[hbres])

            def transpose_tile(hb, hbres, hTo, hTores, tt, first):
                for half in range(2):
                    b = nbank()
                    bb = psum[b][:].bitcast(BF16)

                    def emit(e, half=half, bb=bb):
                        ins = None
                        for j in range(8):
                            kc = half * 8 + j
                            ins = e.transpose(bb[:, j * 128:(j + 1) * 128], hb[:, kc * 128:(kc + 1) * 128], identb[:])
                        return ins
                    P.op("pe", emit, reads=[hbres, constres], writes=[pres[b]])
                    eng = "act" if half == 0 else "dve"
                    dst = hTo[:, half * 8:(half + 1) * 8, tt * 128:(tt + 1) * 128]
                    src = bb[:, 0:1024].rearrange("p (j t) -> p j t", j=8)
                    if eng == "act":
                        P.op("act", lambda e, dst=dst, src=src: e.activation(dst, src, AF.Copy), reads=[pres[b]], writes=[hTores],
                             partial=not (first and half == 0))
                    else:
                        P.op("dve", lambda e, dst=dst, src=src: e.tensor_copy(dst, src), reads=[pres[b]], writes=[hTores],
                             partial=not (first and half == 0))

            def load_bcast(dst, src_row, res):
                P.dma("sp", dst, src_row.partition_broadcast(128), writes=[res])

            st_rr = [0]
            stres = [P.R() for _ in range(4)]
            A_stats = [None]

            A.reset()
            A_stats[0] = A.alloc([4, 5, 6], F32)
            gB = A.alloc([D], F32)
            bB = A.alloc([D], F32)
            gbres = P.R()
            load_bcast(gB, lnin_d[0:1, :], gbres)
            P.dma("sp", bB, lnin_d[1:2, :].partition_broadcast(128), writes=[gbres], partial=True)
            xv = [A.alloc([4, D], F32) for _ in range(2)]
            xvres = [[P.R() for _ in range(4)] for _ in range(2)]
            hbs = [A.alloc([D], BF16) for _ in range(2)]
            hbres = [P.R() for _ in range(2)]
            hTo = [A.alloc([16, TB], BF16) for _ in range(2)]
            hTores = [P.R() for _ in range(2)]
            x_v = x_d.rearrange("(b t p) d -> b p t d", t=4, p=128)
            h32_v = h32_d.rearrange("(b t p) d -> b p t d", t=4, p=128)
            out_v = out_d.rearrange("(b t p) d -> b p t d", t=4, p=128)
            kk = 0
            for tb in range(NTB):
                sl = tb % 2
                P.dma("sp", xv[sl], x_v[tb], writes=xvres[sl])
                for tt in range(4):
                    v = xv[sl][:, tt, :]
                    hsl = kk % 2
                    kk += 1
                    ln_tile(v, xvres[sl][tt], gB, bB, gbres, hbs[hsl], hbres[hsl])
                    transpose_tile(hbs[hsl], hbres[hsl], hTo[sl], hTores[sl], tt, first=(tt == 0))
                P.dma("pool", h32_v[tb], xv[sl], reads=xvres[sl])
                P.dma("pool", hT_d[tb], hTo[sl].rearrange("p a b -> p (a b)"), reads=[hTores[sl]])
            P.barrier()
            if stop_after == "ln0":
                n_layers = 0

            for l in range(n_layers):
                wl = wb_d[l]

                def wv(key, off, n, pat=None, **kw):
                    ap = wl[OFF[key] + off:OFF[key] + off + n]
                    return ap.rearrange(pat, **kw)

                A.reset()
                hTt = [A.alloc([16, TB], BF16) for _ in range(2)]
                hTres = [P.R() for _ in range(2)]
                cst1 = [A.alloc([TB], F32) for _ in range(2)]
                cst2 = [A.alloc([TB], F32) for _ in range(2)]
                cres = [P.R() for _ in range(2)]
                wch = [A.alloc([16, 128], BF16) for _ in range(3)]
                wchres = [P.R() for _ in range(3)]
                wuqn = A.alloc([8, 4, 128], BF16)
                wuqr = A.alloc([4, 4, 256], BF16)
                wuk = A.alloc([8, 4, 128], BF16)
                wuv = A.alloc([4, 1024], BF16)
                colt = A.alloc([NCOL], F32)
                wp1res = P.R()
                cf = A.alloc([8, TB], F32)
                cfres = [P.R() for _ in range(8)]
                sq = A.alloc([8, TB], BF16)
                sqres = [P.R() for _ in range(8)]
                cn = A.alloc([8, TB], BF16)
                cnres = [P.R(), P.R()]
                rbc = [A.alloc([TB], F32) for _ in range(2)]
                rbcres = [P.R(), P.R()]
                stf = [A.alloc([TB], F32) for _ in range(3)]
                stfres = [P.R() for _ in range(3)]
                stb = [A.alloc([TB], BF16) for _ in range(4)]
                stbres = [P.R() for _ in range(4)]
                rt1 = [A.alloc([TB], F32) for _ in range(2)]
                rt2 = [A.alloc([TB], F32) for _ in range(2)]
                rtres = [P.R() for _ in range(2)]
                vst = [A.alloc([4, 1024], BF16) for _ in range(2)]
                vstres = [P.R() for _ in range(2)]
                P.dma("sp", wuqn.rearrange("p a b c -> p (a b c)"), wv("uqn", 0, 524288, "(p f) -> p f", p=128), reads=[wres[l]], writes=[wp1res])
                P.dma("sp", wuqr.rearrange("p a b c -> p (a b c)"), wv("uqr", 0, 524288, "(p f) -> p f", p=128), reads=[wres[l]], writes=[wp1res], partial=True)
                P.dma("sp", wuk.rearrange("p a b c -> p (a b c)"), wv("uk", 0, 524288, "(p f) -> p f", p=128), reads=[wres[l]], writes=[wp1res], partial=True)
                P.dma("sp", wuv.rearrange("p a b -> p (a b)"), wv("uv", 0, 524288, "(p f) -> p f", p=128), reads=[wres[l]], writes=[wp1res], partial=True)
                P.dma("sp", colt, cols_d[l], writes=[wp1res], partial=True)

                seq = [(tb, ch) for tb in range(NTB) for ch in range(25)]

                def p1_load_w(i):
                    if i >= len(seq):
                        return
                    tb, ch = seq[i]
                    sl = i % 3
                    P.dma("sp", wch[sl].rearrange("p a b -> p (a b)"),
                          wv("w_in", ch * 262144, 262144, "(p f) -> p f", p=128), reads=[wres[l]], writes=[wchres[sl]])

                def p1_load_tb(tb):
                    if tb >= NTB:
                        return
                    sl = tb % 2
                    P.dma("sp", hTt[sl].rearrange("p a b -> p (a b)"), hT_d[tb], writes=[hTres[sl]])
                    P.dma("sp", cst1[sl], cs_d[0, :, tb * TB:(tb + 1) * TB], writes=[cres[sl]])
                    P.dma("sp", cst2[sl], cs_d[1, :, tb * TB:(tb + 1) * TB], writes=[cres[sl]], partial=True)

                cnt = {"stf": 0, "stb": 0, "rt": 0}

                def rope_combine(bA, bB_, npart, c1, c2, cr, dst_dram):
                    k = cnt["rt"] % 2
                    cnt["rt"] += 1
                    a1 = rt1[k][0:npart, :]
                    a2 = rt2[k][0:npart, :]
                    P.op("dve", lambda e: e.tensor_tensor(a1, psum[bA][0:npart, :], c1[0:npart, :], ALU.mult),
                         reads=[pres[bA], cr], writes=[rtres[k]])
                    P.op("dve", lambda e: e.tensor_tensor(a2, psum[bB_][0:npart, :], c2[0:npart, :], ALU.mult),
                         reads=[pres[bB_], cr], writes=[rtres[k]], partial=True)
                    sb = cnt["stb"] % 4
                    cnt["stb"] += 1
                    o = stb[sb][0:npart, :]
                    P.op("pool", lambda e: e.tensor_tensor(o, a1, a2, ALU.add), reads=[rtres[k]], writes=[stbres[sb]])
                    P.dma("pool", dst_dram, o, reads=[stbres[sb]])

                def evac_bf16_store(b, dst_dram):
                    sb = cnt["stb"] % 4
                    cnt["stb"] += 1
                    o = stb[sb]
                    P.op("act", lambda e: e.activation(o, psum[b][:], AF.Copy), reads=[pres[b]], writes=[stbres[sb]])
                    P.dma("pool", dst_dram, o, reads=[stbres[sb]])

                p1_load_tb(0)
                p1_load_w(0)
                p1_load_w(1)
                for i, (tb, ch) in enumerate(seq):
                    sl2 = tb % 2
                    tsl = slice(tb * TB, (tb + 1) * TB)
                    if ch == 0:
                        p1_load_tb(tb + 1)
                    p1_load_w(i + 2)
                    ws = i % 3
                    h_in = hTt[sl2]
                    if ch != 8:
                        b = nbank()

                        def emit(e, ws=ws, b=b, h_in=h_in):
                            ins = None
                            for kc in range(16):
                                ins = e.matmul(psum[b][:], wch[ws][:, kc, :], h_in[:, kc, :], start=(kc == 0), stop=(kc == 15))
                            return ins
                        P.op("pe", emit, reads=[wchres[ws], hTres[sl2]], writes=[pres[b]])
                        if ch < 8:
                            P.op("act", lambda e, b=b, ch=ch: e.activation(cf[:, ch, :], psum[b][:], AF.Copy),
                                 reads=[pres[b]], writes=[cfres[ch]])
                            P.op("act", lambda e, b=b, ch=ch: e.activation(sq[:, ch, :], psum[b][:], AF.Square),
                                 reads=[pres[b]], writes=[sqres[ch]])
                        else:
                            k = cnt["stf"] % 3
                            cnt["stf"] += 1
                            P.op("act", lambda e, b=b, k=k: e.activation(stf[k], psum[b][:], AF.Copy),
                                 reads=[pres[b]], writes=[stfres[k]])
                            P.dma("pool", zl_d[ch - 9, :, tsl], stf[k], reads=[stfres[k]])
                    else:
                        bA = nbank()
                        bB_ = nbank()
                        for half, b in ((0, bA), (1, bB_)):
                            def emit(e, ws=ws, b=b, h_in=h_in, half=half):
                                ins = None
                                for kc in range(16):
                                    ins = e.matmul(psum[b][0:64, :], wch[ws][:, kc, half * 64:(half + 1) * 64], h_in[:, kc, :],
                                                   start=(kc == 0), stop=(kc == 15))
                                return ins
                            P.op("pe", emit, reads=[wchres[ws], hTres[sl2]], writes=[pres[b]])
                        rope_combine(bA, bB_, 64, cst1[sl2], cst2[sl2], cres[sl2], kpe_d[:, tsl])
                    if ch == 7:
                        for g in range(2):
                            b = nbank()

                            def emit(e, g=g, b=b):
                                ins = None
                                for j in range(4):
                                    ins = e.matmul(psum[b][:], onesr[:], sq[:, g * 4 + j, :], start=(j == 0), stop=(j == 3))
                                return ins
                            P.op("pe", emit, reads=sqres[g * 4:g * 4 + 4] + [constres], writes=[pres[b]])
                            P.op("act", lambda e, g=g, b=b: e.activation(rbc[g], psum[b][:], AF.Sqrt, bias=RMS_EPS, scale=1.0),
                                 reads=[pres[b]], writes=[rbcres[g]])
                            P.op("dve", lambda e, g=g: e.reciprocal(rbc[g], rbc[g]), reads=[rbcres[g]], writes=[rbcres[g]])
                            for j in range(4):
                                c = g * 4 + j
                                P.op("dve", lambda e, g=g, c=c: e.scalar_tensor_tensor(cn[:, c, :], cf[:, c, :], colt[:, c:c + 1],
                                                                                     rbc[g], ALU.mult, ALU.mult),
                                     reads=[cfres[c], rbcres[g], wp1res], writes=[cnres[g]], partial=(j > 0))
                    if ch == 24:
                        for h in range(8):
                            b = nbank()

                            def emit(e, h=h, b=b):
                                ins = None
                                for kc in range(4):
                                    ins = e.matmul(psum[b][:], wuqn[:, h, kc, :], cn[:, kc, :], start=(kc == 0), stop=(kc == 3))
                                return ins
                            P.op("pe", emit, reads=[wp1res, cnres[0]], writes=[pres[b]])
                            evac_bf16_store(b, qn_d[h, :, tsl])
                        for pr in range(4):
                            bA = nbank()
                            bB_ = nbank()
                            for half, b in ((0, bA), (1, bB_)):
                                def emit(e, pr=pr, b=b, half=half):
                                    ins = None
                                    for kc in range(4):
                                        ins = e.matmul(psum[b][:], wuqr[:, pr, kc, half * 128:(half + 1) * 128], cn[:, kc, :],
                                                       start=(kc == 0), stop=(kc == 3))
                                    return ins
                                P.op("pe", emit, reads=[wp1res, cnres[0]], writes=[pres[b]])
                            rope_combine(bA, bB_, 128, cst1[sl2], cst2[sl2], cres[sl2], qpe_d[pr * 128:(pr + 1) * 128, tsl])
                        for h in range(8):
                            b = nbank()

                            def emit(e, h=h, b=b):
                                ins = None
                                for kc in range(4):
                                    ins = e.matmul(psum[b][:], wuk[:, h, kc, :], cn[:, 4 + kc, :], start=(kc == 0), stop=(kc == 3))
                                return ins
                            P.op("pe", emit, reads=[wp1res, cnres[1]], writes=[pres[b]])
                            evac_bf16_store(b, kn_d[h, :, tsl])
                        vs = tb % 2
                        for tt in range(4):
                            for half in range(2):
                                b = nbank()

                                def emit(e, tt=tt, half=half, b=b):
                                    ins = None
                                    for kc in range(4):
                                        ins = e.matmul(psum[b][:], cn[:, 4 + kc, tt * 128:(tt + 1) * 128],
                                                       wuv[:, kc, half * 512:(half + 1) * 512], start=(kc == 0), stop=(kc == 3))
                                    return ins
                                P.op("pe", emit, reads=[wp1res, cnres[1]], writes=[pres[b]])
                                eng = "act" if half == 0 else "dve"
                                dst = vst[vs][:, tt, half * 512:(half + 1) * 512]
                                first = (tt == 0 and half == 0)
                                if eng == "act":
                                    P.op("act", lambda e, dst=dst, b=b: e.activation(dst, psum[b][:], AF.Copy), reads=[pres[b]],
                                         writes=[vstres[vs]], partial=not first)
                                else:
                                    P.op("dve", lambda e, dst=dst, b=b: e.tensor_copy(dst, psum[b][:]), reads=[pres[b]],
                                         writes=[vstres[vs]], partial=not first)
                        v4 = v_d.rearrange("h p (c d) -> p h c d", d=128)
                        vs4 = vst[vs].rearrange("p c (h d) -> p h c d", h=8)
                        for tt in range(4):
                            P.dma("pool", v4[:, :, tb * 4 + tt, :], vs4[:, :, tt, :], reads=[vstres[vs]])
                P.barrier()
                if stop_after == "p1":
                    break

                A.reset()
                kpe_t = A.alloc([S], BF16)
                kperes = P.R()
                hd = []
                for _ in range(2):
                    hd.append(dict(kn=A.alloc([S], BF16), qn=A.alloc([S], BF16), qpe=A.alloc([S], BF16),
                                   v=A.alloc([32, 128], BF16), res=P.R()))
                pt = [A.alloc([TB], BF16) for _ in range(4)]
                ptres = [P.R() for _ in range(4)]
                rden = [A.alloc([TB], F32) for _ in range(2)]
                rdres = [P.R() for _ in range(2)]
                ost = [A.alloc([TB], BF16) for _ in range(2)]
                ostres = [P.R() for _ in range(2)]
                P.dma("sp", kpe_t[0:64, :], kpe_d, writes=[kperes])

                def p2_load_head(h):
                    if h >= 8:
                        return
                    t = hd[h % 2]
                    P.dma("sp", t["kn"], kn_d[h], writes=[t["res"]])
                    P.dma("sp", t["qn"], qn_d[h], writes=[t["res"]], partial=True)
                    P.dma("sp", t["qpe"][0:64, :], qpe_d[h * 64:(h + 1) * 64, :], writes=[t["res"]], partial=True)
                    P.dma("sp", t["v"].rearrange("p a b -> p (a b)"), v_d[h], writes=[t["res"]], partial=True)

                steps = [(h, qb, kc) for h in range(8) for qb in range(NTB) for kc in range(32)]

                def qk(i):
                    if i >= len(steps):
                        return
                    h, qb, kc = steps[i]
                    t = hd[h % 2]
                    b = i % 4

                    def emit(e, t=t, b=b, qb=qb, kc=kc):
                        e.matmul(psum[b][:], t["kn"][:, kc * 128:(kc + 1) * 128], t["qn"][:, qb * TB:(qb + 1) * TB], start=True, stop=False)
                        return e.matmul(psum[b][:], kpe_t[0:64, kc * 128:(kc + 1) * 128], t["qpe"][0:64, qb * TB:(qb + 1) * TB],
                                        start=False, stop=True)
                    P.op("pe", emit, reads=[t["res"], kperes], writes=[pres[b]])

                p2_load_head(0)
                p2_load_head(1)
                qk(0)
                qk(1)
                for i, (h, qb, kc) in enumerate(steps):
                    t = hd[h % 2]
                    b = i % 4
                    P.op("act", lambda e, b=b: e.activation(pt[b], psum[b][:], AF.Exp, scale=SCALE), reads=[pres[b]], writes=[ptres[b]])
                    qk(i + 2)
                    q2 = (h * NTB + qb) % 2
                    bo = 4 + q2
                    bd = 6 + q2

                    def emit(e, t=t, b=b, kc=kc, bo=bo, bd=bd):
                        e.matmul(psum[bo][:], t["v"][:, kc, :], pt[b], start=(kc == 0), stop=(kc == 31))
                        return e.matmul(psum[bd][:], ones1[:], pt[b], start=(kc == 0), stop=(kc == 31))
                    P.op("pe", emit, reads=[ptres[b], t["res"], constres], writes=[pres[bo], pres[bd]], partial=(kc > 0))
                    if kc == 31:
                        P.op("dve", lambda e, q2=q2, bd=bd: e.reciprocal(rden[q2], psum[bd][:]), reads=[pres[bd]], writes=[rdres[q2]])
                        P.op("dve", lambda e, q2=q2, bo=bo: e.tensor_tensor(ost[q2], psum[bo][:], rden[q2], ALU.mult),
                             reads=[pres[bo], rdres[q2]], writes=[ostres[q2]])
                        P.dma("pool", yT_d[qb, :, h * TB:(h + 1) * TB], ost[q2], reads=[ostres[q2]])
                        if qb == NTB - 1:
                            p2_load_head(h + 2)
                P.barrier()
                if stop_after == "p2":
                    break

                A.reset()
                xl = [A.alloc([S], F32) for _ in range(2)]
                gl = [A.alloc([S], F32) for _ in range(2)]
                xgres = [P.R() for _ in range(2)]
                xc = A.alloc([S], F32)
                xcres = P.R()
                ra = A.alloc([S], F32)
                rares = P.R()
                iu = A.alloc([S], F32)
                iures = P.R()
                s_t = A.alloc([S], F32)
                sres = P.R()
                hh = [A.alloc([S], F32) for _ in range(2)]
                hhres = [P.R() for _ in range(2)]
                xcb = A.alloc([S], BF16)
                xcbres = P.R()
                ybf = A.alloc([S], BF16)
                ybfres = P.R()
                wg = [A.alloc([4, 128], BF16) for _ in range(2)]
                colt3 = A.alloc([NCOL], F32)
                spc = A.alloc([16], F32)
                c3res = P.R()
                spres = P.R()
                P.dma("sp", colt3, cols_d[l], writes=[c3res])
                P.op("act", lambda e: e.activation(spc, colt3[:, 80:96], AF.Exp, scale=-1.0), reads=[c3res], writes=[spres])
                P.op("act", lambda e: e.activation(spc, spc, AF.Ln, bias=1.0, scale=1.0), reads=[spres], writes=[spres])
                P.op("dve", lambda e: e.tensor_scalar(spc, spc, -LRU_C, None, ALU.mult), reads=[spres], writes=[spres])

                def p3_load(bk):
                    if bk >= 8:
                        return
                    sl = bk % 2
                    P.dma("sp", xl[sl], zl_d[bk], writes=[xgres[sl]])
                    P.dma("sp", gl[sl], zl_d[8 + bk], writes=[xgres[sl]], partial=True)
                    P.dma("sp", wg[sl].rearrange("p a b -> p (a b)"), wv("rg", bk * 65536, 65536, "(p f) -> p f", p=128),
                          reads=[wres[l]], writes=[xgres[sl]], partial=True)

                p3_load(0)
                for bk in range(8):
                    p3_load(bk + 1)
                    sl = bk % 2
                    X = xl[sl]
                    G = gl[sl]
                    xr = xgres[sl]

                    def col(i):
                        return colt3[:, i:i + 1]
                    cw = 8 + bk * 4
                    P.op("dve", lambda e, X=X, cw=cw, bk=bk: e.tensor_scalar(xc, X, col(cw + 2), col(40 + bk), ALU.mult, ALU.add),
                         reads=[xr, c3res], writes=[xcres])
                    P.op("dve", lambda e, X=X, cw=cw: e.scalar_tensor_tensor(xc[:, 2:S], X[:, 0:S - 2], col(cw + 0), xc[:, 2:S], ALU.mult, ALU.add),
                         reads=[xr, c3res, xcres], writes=[xcres])
                    P.op("dve", lambda e, X=X, cw=cw: e.scalar_tensor_tensor(xc[:, 1:S], X[:, 0:S - 1], col(cw + 1), xc[:, 1:S], ALU.mult, ALU.add),
                         reads=[xr, c3res, xcres], writes=[xcres])
                    P.op("dve", lambda e, X=X, cw=cw: e.scalar_tensor_tensor(xc[:, 0:S - 1], X[:, 1:S], col(cw + 3), xc[:, 0:S - 1], ALU.mult, ALU.add),
                         reads=[xr, c3res, xcres], writes=[xcres])
                    P.op("act", lambda e: e.activation(xcb, xc, AF.Copy), reads=[xcres], writes=[xcbres])
                    for d in range(2):
                        for ai, (dst, dres) in enumerate(((ra, rares), (iu, iures))):
                            for tb in range(NTB):
                                b = nbank()
                                P.op("pe", lambda e, b=b, sl=sl, d=d, ai=ai, tb=tb: e.matmul(
                                    psum[b][:], wg[sl][:, d * 2 + ai, :], xcb[:, tb * TB:(tb + 1) * TB], start=True, stop=True),
                                    reads=[xr, xcbres], writes=[pres[b]])
                                P.op("act", lambda e, b=b, dst=dst, d=d, ai=ai, tb=tb, bk=bk: e.activation(
                                    dst[:, tb * TB:(tb + 1) * TB], psum[b][:], AF.Sigmoid, bias=col(48 + bk * 4 + d * 2 + ai), scale=1.0),
                                    reads=[pres[b], c3res], writes=[dres], partial=(tb > 0))
                        sc = spc[:, bk * 2 + d:bk * 2 + d + 1]
                        P.op("act", lambda e, sc=sc: e.activation(ra, ra, AF.Exp, scale=sc), reads=[rares, spres], writes=[rares])
                        P.op("act", lambda e: e.activation(s_t, ra, AF.Square), reads=[rares], writes=[sres])
                        P.op("act", lambda e: e.activation(s_t, s_t, AF.Sqrt, bias=1.0, scale=-1.0), reads=[sres], writes=[sres])
                        P.op("pool", lambda e: e.tensor_tensor(iu, iu, xc, ALU.mult), reads=[iures, xcres], writes=[iures])
                        P.op("pool", lambda e: e.tensor_tensor(iu, iu, s_t, ALU.mult), reads=[iures, sres], writes=[iures])
                        if d == 0:
                            P.op("dve", lambda e: e.tensor_tensor_scan(hh[0], ra, iu, 0.0, ALU.mult, ALU.add),
                                 reads=[rares, iures], writes=[hhres[0]])
                        else:
                            P.op("dve", lambda e: e.tensor_tensor_scan(hh[1][:, ::-1], ra[:, ::-1], iu[:, ::-1], 0.0, ALU.mult, ALU.add),
                                 reads=[rares, iures], writes=[hhres[1]])
                    P.op("pool", lambda e, G=G: e.tensor_tensor(s_t, G, G, ALU.mult), reads=[xr, sres], writes=[sres])
                    P.op("pool", lambda e: e.tensor_scalar(s_t, s_t, 0.044715, 1.0, ALU.mult, ALU.add), reads=[sres], writes=[sres])
                    P.op("pool", lambda e, G=G: e.tensor_tensor(s_t, s_t, G, ALU.mult), reads=[xr, sres], writes=[sres])
                    P.op("act", lambda e: e.activation(s_t, s_t, AF.Sigmoid, scale=1.5957691216057308), reads=[sres], writes=[sres])
                    P.op("pool", lambda e, G=G: e.tensor_tensor(s_t, s_t, G, ALU.mult), reads=[xr, sres], writes=[sres])
                    P.op("dve", lambda e: e.tensor_tensor(hh[0], hh[0], hh[1], ALU.add), reads=[hhres[0], hhres[1]], writes=[hhres[0]])
                    P.op("dve", lambda e: e.tensor_tensor(ybf, hh[0], s_t, ALU.mult), reads=[hhres[0], sres], writes=[ybfres])
                    ydst = yT_d.rearrange("b p (k t) -> p b k t", k=16)[:, :, 8 + bk, :]
                    P.dma("pool", ydst, ybf.rearrange("p (b t) -> p b t", b=NTB), reads=[ybfres])
                P.barrier()
                if stop_after == "p3":
                    break

                A.reset()
                A_stats[0] = A.alloc([4, 5, 6], F32)
                wout = A.alloc([16, D], BF16)
                woutres = P.R()
                gB = A.alloc([D], F32)
                bB = A.alloc([D], F32)
                gbres = P.R()
                P.dma("sp", wout.rearrange("p a b -> p (a b)"), wv("wout", 0, 4194304, "(p f) -> p f", p=128), reads=[wres[l]], writes=[woutres])
                load_bcast(gB, lnv_d[l, 0:1, :], gbres)
                P.dma("sp", bB, lnv_d[l, 1:2, :].partition_broadcast(128), writes=[gbres], partial=True)
                yTt = [A.alloc([16, TB], BF16) for _ in range(2)]
                yTres = [P.R() for _ in range(2)]
                hv = [A.alloc([D], F32) for _ in range(3)]
                hvres = [P.R() for _ in range(3)]
                hbs = [A.alloc([D], BF16) for _ in range(2)]
                hbres = [P.R() for _ in range(2)]
                hTo = [A.alloc([16, TB], BF16) for _ in range(2)]
                hTores = [P.R() for _ in range(2)]
                h32_t = h32_d.rearrange("(n p) d -> n p d", p=128)

                def p4_load_y(tb):
                    if tb < NTB:
                        P.dma("sp", yTt[tb % 2].rearrange("p a b -> p (a b)"), yT_d[tb], writes=[yTres[tb % 2]])

                def p4_load_h(n):
                    if n < 32:
                        P.dma("sp", hv[n % 3], h32_t[n], writes=[hvres[n % 3]])

                p4_load_y(0)
                p4_load_h(0)
                p4_load_h(1)
                for tb in range(NTB):
                    p4_load_y(tb + 1)
                    sl = tb % 2
                    for tt in range(4):
                        n = tb * 4 + tt
                        p4_load_h(n + 2)
                        v = hv[n % 3]
                        vr = hvres[n % 3]
                        for db in range(4):
                            b = nbank()

                            def emit(e, b=b, sl=sl, tt=tt, db=db):
                                ins = None
                                for kc in range(16):
                                    ins = e.matmul(psum[b][:], yTt[sl][:, kc, tt * 128:(tt + 1) * 128], wout[:, kc, db * 512:(db + 1) * 512],
                                                   start=(kc == 0), stop=(kc == 15))
                                return ins
                            P.op("pe", emit, reads=[yTres[sl], woutres], writes=[pres[b]])
                            P.op("dve", lambda e, b=b, v=v, db=db: e.scalar_tensor_tensor(
                                v[:, db * 512:(db + 1) * 512], v[:, db * 512:(db + 1) * 512], ALPHA, psum[b][:], ALU.mult, ALU.add),
                                reads=[pres[b], vr], writes=[vr])
                        hsl = n % 2
                        ln_tile(v, vr, gB, bB, gbres, hbs[hsl], hbres[hsl])
                        transpose_tile(hbs[hsl], hbres[hsl], hTo[sl], hTores[sl], tt, first=(tt == 0))
                        P.dma("pool", h32_t[n], v, reads=[vr])
                    P.dma("pool", hT_d[tb], hTo[sl].rearrange("p a b -> p (a b)"), reads=[hTores[sl]])
                P.barrier()
                if stop_after == "p4":
                    break

                last = (l == L - 1)
                A.reset()
                A_stats[0] = A.alloc([4, 5, 6], F32)
                gB = A.alloc([D], F32)
                bB = A.alloc([D], F32)
                gbres = P.R()
                load_bcast(gB, lnv_d[l, 2:3, :], gbres)
                P.dma("sp", bB, lnv_d[l, 3:4, :].partition_broadcast(128), writes=[gbres], partial=True)
                hTt5 = A.alloc([16, TB], BF16)
                hT5res = P.R()
                aT = A.alloc([64, TB], BF16)
                aTres = [P.R() for _ in range(8)]
                wup = [A.alloc([16, 128], BF16) for _ in range(3)]
                wupres = [P.R() for _ in range(3)]
                wdn = [A.alloc([8, 512], BF16) for _ in range(3)]
                wdnres = [P.R() for _ in range(3)]
                rl = [A.alloc([TB], F32) for _ in range(2)]
                rlres = [P.R() for _ in range(2)]
                vv = A.alloc([4, D], F32)
                vvres = [P.R() for _ in range(4)]
                hbs = [A.alloc([D], BF16) for _ in range(2)]
                hbres = [P.R() for _ in range(2)]
                hTo5 = A.alloc([16, TB], BF16)
                hTo5res = P.R()

                useq = [(tb, fc) for tb in range(NTB) for fc in range(64)]
                dseq = [(tb, db, fg) for tb in range(NTB) for db in range(4) for fg in range(8)]

                def p5_load_up(i):
                    if i < len(useq):
                        tb, fc = useq[i]
                        P.dma("sp", wup[i % 3].rearrange("p a b -> p (a b)"), wv("wup", fc * 262144, 262144, "(p f) -> p f", p=128),
                              reads=[wres[l]], writes=[wupres[i % 3]])

                def p5_load_dn(i):
                    if i < len(dseq):
                        tb, db, fg = dseq[i]
                        P.dma("sp", wdn[i % 3].rearrange("p a b -> p (a b)"),
                              wv("wdn", (db * 8 + fg) * 524288, 524288, "(p f) -> p f", p=128),
                              reads=[wres[l]], writes=[wdnres[i % 3]])

                ui = 0
                di = 0
                p5_load_up(0)
                p5_load_up(1)
                p5_load_dn(0)
                p5_load_dn(1)
                for tb in range(NTB):
                    P.dma("sp", hTt5.rearrange("p a b -> p (a b)"), hT_d[tb], writes=[hT5res])
                    P.dma("sp", vv, h32_v[tb], writes=vvres)
                    for fc in range(64):
                        p5_load_up(ui + 2)
                        ws = ui % 3
                        ui += 1
                        b = nbank()

                        def emit(e, b=b, ws=ws):
                            ins = None
                            for kc in range(16):
                                ins = e.matmul(psum[b][:], wup[ws][:, kc, :], hTt5[:, kc, :], start=(kc == 0), stop=(kc == 15))
                            return ins
                        P.op("pe", emit, reads=[wupres[ws], hT5res], writes=[pres[b]])
                        k = fc % 2
                        P.op("act", lambda e, b=b, k=k: e.activation(rl[k], psum[b][:], AF.Relu), reads=[pres[b]], writes=[rlres[k]])
                        P.op("pool", lambda e, k=k, fc=fc: e.tensor_tensor(aT[:, fc, :], rl[k], rl[k], ALU.mult),
                             reads=[rlres[k]], writes=[aTres[fc // 8]], partial=(fc % 8 != 0))
                    for db in range(4):
                        banks = [nbank() for _ in range(4)]
                        for fg in range(8):
                            p5_load_dn(di + 2)
                            ws = di % 3
                            di += 1
                            for tt in range(4):
                                b = banks[tt]

                                def emit(e, b=b, ws=ws, tt=tt, fg=fg):
                                    ins = None
                                    for j in range(8):
                                        ins = e.matmul(psum[b][:], aT[:, fg * 8 + j, tt * 128:(tt + 1) * 128], wdn[ws][:, j, :],
                                                       start=(fg == 0 and j == 0), stop=(fg == 7 and j == 7))
                                    return ins
                                P.op("pe", emit, reads=[aTres[fg], wdnres[ws]], writes=[pres[b]], partial=(fg > 0))
                        for tt in range(4):
                            b = banks[tt]
                            P.op("dve", lambda e, b=b, tt=tt, db=db: e.scalar_tensor_tensor(
                                vv[:, tt, db * 512:(db + 1) * 512], vv[:, tt, db * 512:(db + 1) * 512], ALPHA, psum[b][:], ALU.mult, ALU.add),
                                reads=[pres[b], vvres[tt]], writes=[vvres[tt]])
                    for tt in range(4):
                        hsl = tt % 2
                        if last:
                            ln_tile(vv[:, tt, :], vvres[tt], gB, bB, gbres, None, None)
                        else:
                            ln_tile(vv[:, tt, :], vvres[tt], gB, bB, gbres, hbs[hsl], hbres[hsl])
                            transpose_tile(hbs[hsl], hbres[hsl], hTo5, hTo5res, tt, first=(tt == 0))
                    if last:
                        P.dma("pool", out_v[tb], vv, reads=vvres)
                    else:
                        P.dma("pool", h32_v[tb], vv, reads=vvres)
                        P.dma("pool", hT_d[tb], hTo5.rearrange("p a b -> p (a b)"), reads=[hTo5res])
                P.barrier()

        P.barrier()
        block = es.enter_context(nc.Block())

        @block.tensor
        def _(e):
            P.replay("pe", e)

        @block.scalar
        def _(e):
            P.replay("act", e)

        @block.vector
        def _(e):
            P.replay("dve", e)

        @block.gpsimd
        def _(e):
            P.replay("pool", e)

        @block.sync
        def _(e):
            P.replay("sp", e)
    return nc


def prep_inputs(inp, ncores=NCORES):
    f = np.float32
    w_in = np.asarray(inp["w_in"], f)
    kpe = w_in[:, :, 1024:1088]
    kpe_sw = np.concatenate([kpe[:, :, 32:], kpe[:, :, :32]], -1)
    w_in_p = np.concatenate([w_in[:, :, :1024], kpe, kpe_sw, w_in[:, :, 1088:]], -1)
    w_in_t = w_in_p.reshape(L, 16, 128, 25, 128).transpose(0, 3, 2, 1, 4)
    w_uq = np.asarray(inp["w_uq"], f).reshape(L, 4, 128, 8, 192)
    uqn = w_uq[..., :128].transpose(0, 2, 3, 1, 4)
    rope = w_uq[..., 128:]
    rope_sw = np.concatenate([rope[..., 32:], rope[..., :32]], -1)
    ra = rope.reshape(L, 4, 128, 4, 128)
    rb = rope_sw.reshape(L, 4, 128, 4, 128)
    uqr = np.concatenate([ra, rb], -1).transpose(0, 2, 3, 1, 4)
    w_ukv = np.asarray(inp["w_ukv"], f).reshape(L, 4, 128, 8, 256)
    uk = w_ukv[..., :128].transpose(0, 2, 3, 1, 4)
    uv = w_ukv[..., 128:].reshape(L, 4, 128, 1024).transpose(0, 2, 1, 3)
    wout = np.asarray(inp["w_out"], f).reshape(L, 16, 128, 2048).transpose(0, 2, 1, 3)
    wup = np.asarray(inp["w_up"], f).reshape(L, 16, 128, 64, 128).transpose(0, 3, 2, 1, 4)
    wdn = np.asarray(inp["w_down"], f).reshape(L, 8, 8, 128, 4, 512).transpose(0, 4, 1, 3, 2, 5)
    rg = np.stack([np.asarray(inp["rg_w_a"], f), np.asarray(inp["rg_w_i"], f)], 2)
    rg = rg.transpose(0, 3, 4, 1, 2, 5).reshape(L, 8, 128, 4, 128)
    wf = np.concatenate([a.reshape(L, -1) for a in (w_in_t, uqn, uqr, uk, uv, wout, wup, wdn, rg)], 1)
    assert wf.shape == (L, NW), wf.shape
    wf = np.ascontiguousarray(wf, dtype=f)

    cols = np.zeros((L, 128, NCOL), f)
    cols[:, :, 0:4] = np.asarray(inp["q_norm_g"], f).reshape(L, 4, 128).transpose(0, 2, 1)
    cols[:, :, 4:8] = np.asarray(inp["kv_norm_g"], f).reshape(L, 4, 128).transpose(0, 2, 1)
    cw = np.asarray(inp["conv_w"], f).reshape(L, 4, 8, 128).transpose(0, 3, 2, 1)
    cols[:, :, 8:40] = cw.reshape(L, 128, 32)
    cols[:, :, 40:48] = np.asarray(inp["conv_b"], f).reshape(L, 8, 128).transpose(0, 2, 1)
    rb_ = np.stack([np.asarray(inp["rg_b_a"], f), np.asarray(inp["rg_b_i"], f)], 2)
    cols[:, :, 48:80] = rb_.transpose(0, 4, 3, 1, 2).reshape(L, 128, 32)
    lam = np.asarray(inp["rg_lambda"], f).reshape(L, 2, 8, 128).transpose(0, 3, 2, 1)
    cols[:, :, 80:96] = lam.reshape(L, 128, 16)
    lnv = np.stack([np.asarray(inp[k], f) for k in ("ln1_g", "ln1_b", "ln2_g", "ln2_b")], 1)
    lnin = np.stack([np.asarray(inp["ln_in_g"], f), np.asarray(inp["ln_in_b"], f)], 0)
    consts = np.zeros((128, 4), f)
    p = np.arange(128)
    inv_freq = (10000.0 ** (-np.arange(0, 64, 2, dtype=np.float32) / np.float32(64))).astype(f)
    consts[:, 0] = inv_freq[p % 32]
    consts[:, 1] = np.where((p % 64) < 32, -1.0, 1.0)
    ident = np.eye(128, dtype=f)
    shared = dict(cols=np.ascontiguousarray(cols), lnv=np.ascontiguousarray(lnv), lnin=np.ascontiguousarray(lnin),
                  consts=consts, ident=ident)
    for l in range(L):
        shared[f"wf{l}"] = wf[l]
    x = np.asarray(inp["x"], f)
    pos = np.asarray(inp["positions"], np.int32)
    nseq = 8 // ncores
    in_maps = []
    for c in range(ncores):
        m = dict(shared)
        m["x"] = np.ascontiguousarray(x[c * nseq:(c + 1) * nseq])
        m["pos"] = np.ascontiguousarray(pos[c * nseq:(c + 1) * nseq, None, :])
        in_maps.append(m)
    return in_maps


def kernel(**inputs):
    in_maps = prep_inputs(inputs, NCORES)
    nc = build_program(nseq=8 // NCORES)
    res = run_bass_kernel_spmd(nc, in_maps, core_ids=list(range(NCORES)))
    outs = [np.asarray(r["out"], np.float32) for r in res.results]
    return np.concatenate(outs, 0)
```
